# Optimizing a Trainium2 kernel written in Bass

```python
import math, functools
import jax, jax.numpy as jnp
from jax import lax
import numpy as np

D_MODEL = 1024
BATCH = 32
SEQ = 2048
DEPTH = 2
DEC_BATCH = 128
DEC_SEQ = 8
PAST_LEN = 16384
PAGE_SIZE = 128

N_MIXERS = 2
CONV_WIDTH = 3
D_CONV = D_MODEL
N_HEADS = 8
QK_NOPE = 128
QK_ROPE = 64
QK_HEAD = QK_NOPE + QK_ROPE
V_HEAD = 128
KV_LORA = D_MODEL // 4
Q_LORA = 3 * D_MODEL // 8
D_MLA = N_HEADS * V_HEAD
ROPE_THETA = 10000.0
N_MEM = 256
MEM_HEADS = 4
MEM_HEAD_DIM = 128
D_MEM = MEM_HEADS * MEM_HEAD_DIM
D_BRANCH_CONV = D_CONV + D_MEM
D_BRANCH_MLA = D_MLA + D_MEM
CONV_SPLITS = [D_CONV, 2 * D_CONV, 3 * D_CONV, 3 * D_CONV + D_MEM]
MLA_SPLITS = [Q_LORA, Q_LORA + KV_LORA, Q_LORA + KV_LORA + QK_ROPE, Q_LORA + KV_LORA + QK_ROPE + D_MEM]
EPS = 1e-6
SM_SCALE = QK_HEAD ** -0.5
MEM_SCALE = MEM_HEAD_DIM ** -0.5
Q_BLOCK = 128
KV_BLOCK_PAGES_MAX = 16

kernel_name = 'hybrid_shortconv_mla_memory_step'


def rms_norm(x, g):
    xf = x.astype(jnp.float32)
    y = xf * lax.rsqrt(jnp.mean(xf * xf, axis=-1, keepdims=True) + EPS)
    return (y * g.astype(jnp.float32)).astype(x.dtype)


def rope_tables(pos):
    inv = 1.0 / (ROPE_THETA ** (jnp.arange(0, QK_ROPE, 2, dtype=jnp.float32) / QK_ROPE))
    ang = pos.astype(jnp.float32)[:, None] * inv[None, :]
    return jnp.cos(ang), jnp.sin(ang)


def apply_rope(x, cos, sin):
    half = x.shape[-1] // 2
    xf = x.astype(jnp.float32)
    x1, x2 = xf[..., :half], xf[..., half:]
    return jnp.concatenate([x1 * cos - x2 * sin, x1 * sin + x2 * cos], axis=-1).astype(x.dtype)


def mem_kv(mem, norm_g, w_kv, k_g):
    b, m, _ = mem.shape
    kv = rms_norm(mem, norm_g) @ w_kv
    k = kv[..., :D_MEM].reshape(b, m, MEM_HEADS, MEM_HEAD_DIM)
    v = kv[..., D_MEM:].reshape(b, m, MEM_HEADS, MEM_HEAD_DIM)
    return rms_norm(k, k_g), v


def mem_attend(q, k, v):
    s = jnp.einsum('bthd,bmhd->bhtm', q, k, preferred_element_type=jnp.float32) * MEM_SCALE
    p = jax.nn.softmax(s, axis=-1)
    o = jnp.einsum('bhtm,bmhd->bthd', p.astype(v.dtype), v)
    return o.reshape(q.shape[0], q.shape[1], D_MEM)


def conv_layer(x, hist, mem_k, mem_v, mem_q_g, norm_g, w_in, conv_w, w_out):
    b, t, _ = x.shape
    proj = rms_norm(x, norm_g) @ w_in
    u, gb, gc, q_mem, z = jnp.split(proj, CONV_SPLITS, axis=-1)
    full = jnp.concatenate([hist.astype(x.dtype), gc * u], axis=1)
    conv = sum(conv_w[j] * full[:, j:j + t] for j in range(CONV_WIDTH))
    y_conv = gb * conv
    q = rms_norm(q_mem.reshape(b, t, MEM_HEADS, MEM_HEAD_DIM), mem_q_g)
    y_mem = mem_attend(q, mem_k, mem_v)
    out = jnp.concatenate([y_conv, y_mem], axis=-1) * jax.nn.silu(z)
    return x + out @ w_out, full[:, t:]


def mla_scores(q_abs, q_pe, ckv, kpe, ks):
    s = (jnp.einsum('bqhc,bkc->bhqk', q_abs, ckv, preferred_element_type=jnp.float32)
         + jnp.einsum('bqhr,bkr->bhqk', q_pe, kpe, preferred_element_type=jnp.float32))
    return s * (jnp.transpose(ks.astype(jnp.float32), (0, 2, 1))[:, :, None, :] * SM_SCALE)


def mla_attend_prompt(q_abs, q_pe, ckv, kpe, ks):
    t = q_abs.shape[1]
    outs = []
    for qb in range(t // Q_BLOCK):
        lo, hi = qb * Q_BLOCK, (qb + 1) * Q_BLOCK
        s = mla_scores(q_abs[:, lo:hi], q_pe[:, lo:hi], ckv[:, :hi], kpe[:, :hi], ks[:, :hi])
        mask = (lo + jnp.arange(Q_BLOCK))[:, None] >= jnp.arange(hi)[None, :]
        p = jax.nn.softmax(jnp.where(mask[None, None], s, -jnp.inf), axis=-1)
        outs.append(jnp.einsum('bhqk,bkc->bqhc', p.astype(ckv.dtype), ckv[:, :hi]))
    return jnp.concatenate(outs, axis=1)


def mla_attend_sample(q_abs, q_pe, ckv, kpe, ks, cache_ckv, cache_kpe, cache_ks, page_table):
    b, t = q_abs.shape[0], q_abs.shape[1]
    s = mla_scores(q_abs, q_pe, ckv, kpe, ks)
    s = jnp.where(jnp.tril(jnp.ones((t, t), dtype=bool))[None, None], s, -jnp.inf)
    m = jnp.max(s, axis=-1)
    p = jnp.exp(s - m[..., None])
    l = jnp.sum(p, axis=-1)
    acc = jnp.einsum('bhqk,bkc->bhqc', p, ckv, preferred_element_type=jnp.float32)
    n_pages = page_table.shape[1]
    ppb = math.gcd(n_pages, KV_BLOCK_PAGES_MAX)
    blocks = jnp.transpose(page_table.reshape(b, n_pages // ppb, ppb), (1, 0, 2))

    def step(carry, pages):
        m, l, acc = carry
        ck = cache_ckv[pages].reshape(b, ppb * PAGE_SIZE, KV_LORA)
        kp = cache_kpe[pages].reshape(b, ppb * PAGE_SIZE, QK_ROPE)
        kk = cache_ks[pages].reshape(b, ppb * PAGE_SIZE, N_HEADS)
        s = mla_scores(q_abs, q_pe, ck, kp, kk)
        m_new = jnp.maximum(m, jnp.max(s, axis=-1))
        alpha = jnp.exp(m - m_new)
        p = jnp.exp(s - m_new[..., None])
        l = l * alpha + jnp.sum(p, axis=-1)
        acc = acc * alpha[..., None] + jnp.einsum('bhqk,bkc->bhqc', p, ck, preferred_element_type=jnp.float32)
        return (m_new, l, acc), None

    (m, l, acc), _ = lax.scan(step, (m, l, acc), blocks)
    return jnp.transpose(acc / l[..., None], (0, 2, 1, 3)).astype(q_abs.dtype)


def mla_layer(x, pos, attend, mem_k, mem_v, mem_q_g, norm_g, w_in, q_lora_g, w_uq, ckv_g, w_uk, w_uv, q_g, k_g, w_out):
    b, t, _ = x.shape
    proj = rms_norm(x, norm_g) @ w_in
    cq, ckv_raw, kpe_raw, q_mem, z = jnp.split(proj, MLA_SPLITS, axis=-1)
    q = (rms_norm(cq, q_lora_g) @ w_uq).reshape(b, t, N_HEADS, QK_HEAD)
    qn = rms_norm(q, q_g)
    ckv = rms_norm(ckv_raw, ckv_g)
    k_nope = jnp.einsum('btc,chd->bthd', ckv, w_uk)
    ms = (jnp.sum(jnp.square(k_nope.astype(jnp.float32)), axis=-1)
          + jnp.sum(jnp.square(kpe_raw.astype(jnp.float32)), axis=-1)[..., None]) / QK_HEAD
    kscale = lax.rsqrt(ms + EPS).astype(x.dtype)
    cos, sin = rope_tables(pos)
    kpe = apply_rope(kpe_raw * k_g[QK_NOPE:], cos, sin)
    q_abs = jnp.einsum('bthd,chd->bthc', qn[..., :QK_NOPE] * k_g[:QK_NOPE], w_uk)
    q_pe = apply_rope(qn[..., QK_NOPE:], cos[:, None, :], sin[:, None, :])
    o_lat = attend(q_abs, q_pe, ckv, kpe, kscale)
    y_mla = jnp.einsum('bthc,chd->bthd', o_lat, w_uv).reshape(b, t, D_MLA)
    qm = rms_norm(q_mem.reshape(b, t, MEM_HEADS, MEM_HEAD_DIM), mem_q_g)
    y_mem = mem_attend(qm, mem_k, mem_v)
    out = jnp.concatenate([y_mla, y_mem], axis=-1) * jax.nn.silu(z)
    return x + out @ w_out, ckv, kpe, kscale


def setup_inputs(seed: int = 0) -> dict:
    key = jax.random.key(seed)
    ks = jax.random.split(key, 32)
    f32 = jnp.float32
    n_pages = PAST_LEN // PAGE_SIZE
    n_used = DEC_BATCH * n_pages
    n_pool = n_used + n_used // 4

    def nrm(k, shape, scale=1.0):
        return jax.random.normal(k, shape, f32) * scale

    def gain(k, shape):
        return 1.0 + 0.02 * jax.random.normal(k, shape, f32)

    page_table = jax.random.permutation(ks[0], n_pool)[:n_used].reshape(DEC_BATCH, n_pages).astype(jnp.int32)
    return {
        'x_prompt': nrm(ks[1], (BATCH, SEQ, D_MODEL)),
        'x_sample': nrm(ks[2], (DEC_BATCH, DEC_SEQ, D_MODEL)),
        'mem_prompt': nrm(ks[3], (BATCH, N_MEM, D_MODEL)),
        'state_conv': nrm(ks[4], (DEC_BATCH, CONV_WIDTH - 1, D_CONV)),
        'cache_ckv': nrm(ks[5], (n_pool, PAGE_SIZE, KV_LORA)),
        'cache_kpe': nrm(ks[6], (n_pool, PAGE_SIZE, QK_ROPE)),
        'cache_kscale': jax.random.uniform(ks[7], (n_pool, PAGE_SIZE, N_HEADS), f32, 0.8, 1.2),
        'cache_mem_k': nrm(ks[8], (DEPTH, DEC_BATCH, N_MEM, MEM_HEADS, MEM_HEAD_DIM)),
        'cache_mem_v': nrm(ks[9], (DEPTH, DEC_BATCH, N_MEM, MEM_HEADS, MEM_HEAD_DIM)),
        'page_table': page_table,
        'conv_norm_g': gain(ks[10], (D_MODEL,)),
        'conv_w_in': nrm(ks[11], (D_MODEL, 3 * D_CONV + D_MEM + D_BRANCH_CONV), D_MODEL ** -0.5),
        'conv_w': nrm(ks[12], (CONV_WIDTH, D_CONV), CONV_WIDTH ** -0.5),
        'conv_w_out': nrm(ks[13], (D_BRANCH_CONV, D_MODEL), D_BRANCH_CONV ** -0.5),
        'mla_norm_g': gain(ks[14], (D_MODEL,)),
        'mla_w_in': nrm(ks[15], (D_MODEL, Q_LORA + KV_LORA + QK_ROPE + D_MEM + D_BRANCH_MLA), D_MODEL ** -0.5),
        'mla_q_lora_g': gain(ks[16], (Q_LORA,)),
        'mla_w_uq': nrm(ks[17], (Q_LORA, N_HEADS * QK_HEAD), Q_LORA ** -0.5),
        'mla_ckv_g': gain(ks[18], (KV_LORA,)),
        'mla_w_uk': nrm(ks[19], (KV_LORA, N_HEADS, QK_NOPE), KV_LORA ** -0.5),
        'mla_w_uv': nrm(ks[20], (KV_LORA, N_HEADS, V_HEAD), KV_LORA ** -0.5),
        'mla_q_g': gain(ks[21], (QK_HEAD,)),
        'mla_k_g': gain(ks[22], (QK_HEAD,)),
        'mla_w_out': nrm(ks[23], (D_BRANCH_MLA, D_MODEL), D_BRANCH_MLA ** -0.5),
        'mem_norm_g': gain(ks[24], (DEPTH, D_MODEL)),
        'mem_w_kv': nrm(ks[25], (DEPTH, D_MODEL, 2 * D_MEM), D_MODEL ** -0.5),
        'mem_q_g': gain(ks[26], (DEPTH, MEM_HEAD_DIM)),
        'mem_k_g': gain(ks[27], (DEPTH, MEM_HEAD_DIM)),
    }


def reference(x_prompt, x_sample, mem_prompt, state_conv, cache_ckv, cache_kpe, cache_kscale, cache_mem_k,
              cache_mem_v, page_table, conv_norm_g, conv_w_in, conv_w, conv_w_out, mla_norm_g, mla_w_in,
              mla_q_lora_g, mla_w_uq, mla_ckv_g, mla_w_uk, mla_w_uv, mla_q_g, mla_k_g, mla_w_out,
              mem_norm_g, mem_w_kv, mem_q_g, mem_k_g):
    seq = x_prompt.shape[1]
    past = page_table.shape[1] * PAGE_SIZE
    pos_p = jnp.arange(seq, dtype=jnp.int32)
    pos_s = past + jnp.arange(x_sample.shape[1], dtype=jnp.int32)
    attend_s = functools.partial(mla_attend_sample, cache_ckv=cache_ckv, cache_kpe=cache_kpe,
                                 cache_ks=cache_kscale, page_table=page_table)
    y_p, y_s = x_prompt, x_sample
    mem_ks, mem_vs = [], []
    for i in range(DEPTH):
        mk_p, mv_p = mem_kv(mem_prompt, mem_norm_g[i], mem_w_kv[i], mem_k_g[i])
        mem_ks.append(mk_p)
        mem_vs.append(mv_p)
        mk_s, mv_s = cache_mem_k[i], cache_mem_v[i]
        if i % N_MIXERS == 0:
            hist_p = jnp.zeros((y_p.shape[0], CONV_WIDTH - 1, D_CONV), y_p.dtype)
            y_p, conv_p = conv_layer(y_p, hist_p, mk_p, mv_p, mem_q_g[i], conv_norm_g, conv_w_in, conv_w, conv_w_out)
            y_s, conv_s = conv_layer(y_s, state_conv, mk_s, mv_s, mem_q_g[i], conv_norm_g, conv_w_in, conv_w, conv_w_out)
        else:
            y_p, ckv_p, kpe_p, ksc_p = mla_layer(y_p, pos_p, mla_attend_prompt, mk_p, mv_p, mem_q_g[i],
                                                 mla_norm_g, mla_w_in, mla_q_lora_g, mla_w_uq, mla_ckv_g,
                                                 mla_w_uk, mla_w_uv, mla_q_g, mla_k_g, mla_w_out)
            y_s, ckv_s, kpe_s, ksc_s = mla_layer(y_s, pos_s, attend_s, mk_s, mv_s, mem_q_g[i],
                                                 mla_norm_g, mla_w_in, mla_q_lora_g, mla_w_uq, mla_ckv_g,
                                                 mla_w_uk, mla_w_uv, mla_q_g, mla_k_g, mla_w_out)
    new_mem_k = jnp.stack(mem_ks)
    new_mem_v = jnp.stack(mem_vs)
    return (y_p, y_s, conv_p, conv_s, ckv_p, kpe_p, ksc_p, ckv_s, kpe_s, ksc_s, new_mem_k, new_mem_v)
```

```python
from contextlib import ExitStack
import numpy as np
import ml_dtypes
import concourse.bass as bass
import concourse.mybir as mybir
from concourse.bass_utils import run_bass_kernel_spmd

F32 = mybir.dt.float32
BF16 = mybir.dt.bfloat16
I32 = mybir.dt.int32
AF = mybir.ActivationFunctionType
ALU = mybir.AluOpType
AX = mybir.AxisListType

D = 1024
NH = 8
NOPE = 128
ROPE = 64
QKH = 192
KVL = 256
QL = 384
NMEM = 256
MH = 4
TS = 8
EPS = 1e-6
SM_SCALE = QKH ** -0.5
MEM_SCALE = 128 ** -0.5
ROPE_THETA = 10000.0
PAGE = 128
TB = 512


class Cfg:
    def __init__(self, nb_p=4, seq=2048, bs_s=16, npages=128, npool=20480, ncores=8):
        self.nb_p, self.seq, self.bs_s, self.npages, self.npool, self.ncores = nb_p, seq, bs_s, npages, npool, ncores
        self.ns = bs_s * TS


class Buf:
    __slots__ = ("w", "r", "excl")

    def __init__(self, excl=False):
        self.w = {}
        self.r = {}
        self.excl = excl


class K:
    def __init__(self, nc, stack):
        self.nc, self.stack = nc, stack
        self.eng = {"pe": nc.tensor, "act": nc.scalar, "dve": nc.vector, "pool": nc.gpsimd, "sp": nc.sync}
        self.sem, self.cnt, self.waited = {}, {}, {}
        for e in self.eng:
            self.sem[e] = stack.enter_context(nc.semaphore("s_" + e))
            self.cnt[e] = 0
        self.dsems = []
        import os as _os
        self.limit = int(_os.environ.get("DBG_LIMIT", "1000000000"))
        self.n = 0

    def alive(self):
        return self.n < self.limit

    def wait(self, e, ev):
        if ev is None:
            return
        sem, val, key = ev
        kk = (e, key)
        if self.waited.get(kk, 0) >= val:
            return
        self.eng[e].wait_ge(sem, val)
        self.waited[kk] = val

    def deps(self, e, reads, writes):
        for b in reads:
            for w in b.w.values():
                self.wait(e, w)
            if b.excl:
                for key, r in b.r.items():
                    if key != e:
                        self.wait(e, r)
        for b in writes:
            for w in b.w.values():
                self.wait(e, w)
            for r in b.r.values():
                self.wait(e, r)

    def done(self, ev, reads, writes):
        for b in reads:
            old = b.r.get(ev[2])
            if old is None or old[1] < ev[1]:
                b.r[ev[2]] = ev
        for b in writes:
            b.w[ev[2]] = ev
            b.r = {}

    def op(self, e, fn, reads=(), writes=()):
        if self.n >= self.limit:
            return None
        self.n += 1
        if self.limit < 1000000000:
            import sys as _sys
            print("OP", self.n, e, "line", _sys._getframe(1).f_lineno, _sys._getframe(2).f_lineno)
        self.deps(e, reads, writes)
        ins = fn()
        self.cnt[e] += 1
        ins.then_inc(self.sem[e], 1)
        ev = (self.sem[e], self.cnt[e], e)
        self.done(ev, reads, writes)
        return ev

    def dsem(self, name):
        s = self.stack.enter_context(self.nc.semaphore("d_" + name))
        d = (s, [0], "d_" + name)
        self.dsems.append(d)
        return d

    def dma(self, q, out, in_, dsem, reads=(), writes=(), **kw):
        if self.n >= self.limit:
            return None
        self.n += 1
        if self.limit < 1000000000:
            import sys as _sys
            print("DMA", self.n, q, "line", _sys._getframe(1).f_lineno, _sys._getframe(2).f_lineno)
        self.deps(q, reads, writes)
        ins = self.eng[q].dma_start(out=out, in_=in_, **kw)
        dsem[1][0] += 16
        ins.then_inc(dsem[0], 16)
        ev = (dsem[0], dsem[1][0], dsem[2])
        self.done(ev, reads, writes)
        return ev

    def barrier(self):
        for e in self.eng:
            self.wait_all_dma(e)
            for o in self.eng:
                if o != e and self.cnt[o] > 0:
                    self.wait(e, (self.sem[o], self.cnt[o], o))

    def wait_all_dma(self, e):
        for s, c, key in self.dsems:
            if c[0] > 0:
                self.wait(e, (s, c[0], key))


def build(cfg, phases=("m", "l0p", "l0s", "l1p", "l1s")):
    NB, S, BS, NS, NPG, NPOOL = cfg.nb_p, cfg.seq, cfg.bs_s, cfg.ns, cfg.npages, cfg.npool
    NBLK = S // TB
    NKT = S // 128
    HT = NH * TS
    nc = bass.Bass("TRN2", target_bir_lowering=False)

    def din(name, shape, dt=F32):
        return nc.dram_tensor(name, list(shape), dt, kind="ExternalInput").ap()

    def dout(name, shape, dt=F32):
        return nc.dram_tensor(name, list(shape), dt, kind="ExternalOutput").ap()

    x_p = din("x_p", [NB, S, D]); x_s = din("x_s", [NS, D]); mem_p = din("mem_p", [NB, NMEM, D])
    st_conv = din("st_conv", [BS * 2, D])
    PW = KVL + ROPE + NH
    c_pool = din("c_pool", [NPOOL, PAGE, PW])
    c_mk = din("c_mk", [2, BS, NMEM, MH * 128]); c_mv = din("c_mv", [2, BS, NMEM, MH * 128])
    ptab = din("ptab", [1, BS * NPG], I32)
    conv_norm_g = din("conv_norm_g", [1, D]); conv_w_in = din("conv_w_in", [D, 5120]); conv_w = din("conv_w", [3, D])
    conv_w_out = din("conv_w_out", [1536, D]); mla_norm_g = din("mla_norm_g", [1, D]); mla_w_in = din("mla_w_in", [D, 2752])
    mla_q_lora_g = din("mla_q_lora_g", [1, QL]); mla_w_uq = din("mla_w_uq", [QL, 1536]); mla_ckv_g = din("mla_ckv_g", [1, KVL])
    mla_w_uk = din("mla_w_uk", [KVL, 1024]); mla_w_uv = din("mla_w_uv", [KVL, 1024])
    mla_q_g = din("mla_q_g", [1, QKH]); mla_k_g = din("mla_k_g", [1, QKH]); mla_w_out = din("mla_w_out", [1536, D])
    mem_norm_g = din("mem_norm_g", [2, D]); mem_w_kv = din("mem_w_kv", [2, D, 1024])
    mem_q_g = din("mem_q_g", [2, 128]); mem_k_g = din("mem_k_g", [2, 128])
    c_ident = din("c_ident", [128, 128], BF16); c_identf = din("c_identf", [128, 128])
    c_mask = din("c_mask", [128, 128], BF16)
    c_rot = din("c_rot", [64, 64], BF16)
    c_csA_p = din("c_csA_p", [S, 64]); c_csB_p = din("c_csB_p", [S, 64])
    c_csA_s = din("c_csA_s", [NS, 64]); c_csB_s = din("c_csB_s", [NS, 64])
    c_cosT_p = din("c_cosT_p", [64, S]); c_sinT_p = din("c_sinT_p", [64, S])
    c_cosT_s = din("c_cosT_s", [64, NS]); c_sinT_s = din("c_sinT_s", [64, NS])
    c_smask = din("c_smask", [NS, NS], BF16)

    y_p = dout("y_p", [NB, S, D]); y_s = dout("y_s", [NS, D]); nconv_p = dout("nconv_p", [NB * 2, D]); nconv_s = dout("nconv_s", [BS * 2, D])
    ckv_p = dout("ckv_p", [NB, S, KVL]); kpe_p = dout("kpe_p", [NB, S, ROPE]); ksc_p = dout("ksc_p", [NB, S, NH])
    ckv_s = dout("ckv_s", [NS, KVL]); kpe_s = dout("kpe_s", [NS, ROPE]); ksc_s = dout("ksc_s", [NS, NH])
    nmk_p = dout("nmk_p", [2, NB, NMEM, MH * 128]); nmv_p = dout("nmv_p", [2, NB, NMEM, MH * 128])
    y1_d = nc.dram_tensor("y1_scratch", [NB, S, D], F32, kind="Internal").ap()
    kt_d = nc.dram_tensor("kt_scratch", [2, NB, 128, MH * NMEM], BF16, kind="Internal").ap()
    vb_d = nc.dram_tensor("vb_scratch", [2, NB, 128, 2 * 512], BF16, kind="Internal").ap()

    with ExitStack() as st:
        k = K(nc, st)
        V, A, P, T = nc.vector, nc.scalar, nc.gpsimd, nc.tensor

        def sb(stack, name, shape, dt):
            return stack.enter_context(nc.sbuf_tensor(name, list(shape), dt))

        ident = sb(st, "ident", [128, 128], BF16); identf = sb(st, "identf", [128, 128], F32)
        ones = sb(st, "ones", [128, 128], BF16); mask = sb(st, "mask", [128, 128], BF16)
        rot = sb(st, "rot", [64, 64], BF16)
        mh = sb(st, "mh", [128, 1], F32); epsb = sb(st, "epsb", [128, 1], F32); oneb = sb(st, "oneb", [128, 1], F32)
        Bc = Buf()
        pt = [st.enter_context(nc.psum_tensor("pt%d" % i, [128, 1024], BF16)) for i in range(2)]
        Bpt = [Buf(True), Buf(True)]
        pf = [st.enter_context(nc.psum_tensor("pf%d" % i, [128, 512], F32)) for i in range(6)]
        Bpf = [Buf(True) for _ in range(6)]
        st.enter_context(nc.Block())
        dc = k.dsem("const")
        k.dma("sp", ident[:], c_ident[:, :], dc, writes=[Bc])
        k.dma("sp", identf[:], c_identf[:, :], dc, writes=[Bc])
        k.dma("sp", mask[:], c_mask[:, :], dc, writes=[Bc])
        k.dma("sp", rot[:], c_rot[:, :], dc, writes=[Bc])
        k.op("pool", lambda: P.memset(ones[:], 1.0), writes=[Bc])
        k.op("pool", lambda: P.memset(mh[:], -0.5), writes=[Bc])
        k.op("pool", lambda: P.memset(epsb[:], EPS), writes=[Bc])
        k.op("pool", lambda: P.memset(oneb[:], 1.0), writes=[Bc])
        for e in ("pe", "act", "dve", "pool", "sp"):
            k.deps(e, [Bc], [])

        xt = [sb(st, "xt%d" % i, [128, D], F32) for i in range(2)]; Bxt = [Buf(), Buf()]; dxt = [k.dsem("xt0"), k.dsem("xt1")]
        xn = [sb(st, "xn%d" % i, [128, D], BF16) for i in range(2)]; Bxn = [Buf(), Buf()]
        sm = [sb(st, "sm%d" % i, [128, 16], F32) for i in range(2)]; Bsm = [Buf(), Buf()]
        y1s = sb(st, "y1s", [128, D], F32); By1s = Buf(); dy1s = k.dsem("y1s")
        fe = {"i": 0}

        def mmg(ps, Bps, pairs, reads, start=True, stop=True, skip=False):
            if not k.alive():
                return None
            k.deps("pe", reads, [Bps])
            n = len(pairs)
            kw = {"skip_group_check": True} if skip else {}
            for i, (l, r) in enumerate(pairs[:-1]):
                T.matmul(ps, lhsT=l, rhs=r, start=(start and i == 0), stop=False, **kw)
            l, r = pairs[-1]
            return k.op("pe", lambda: T.matmul(ps, lhsT=l, rhs=r, start=(start and n == 1), stop=stop, **kw), reads=reads, writes=[Bps])

        def rsq(out_ap, in_ap, n_inv, reads, writes, bias_ap=None):
            npart = out_ap.shape[0]
            bb_ = bias_ap if bias_ap is not None else epsb[:npart, 0:1]
            k.op("act", lambda: A.activation(out=out_ap, in_=in_ap, func=AF.Ln, scale=n_inv, bias=bb_), reads=reads, writes=writes)
            k.op("act", lambda: A.activation(out=out_ap, in_=out_ap, func=AF.Exp, scale=-0.5), reads=writes, writes=writes)

        def rstd_chain(ssq_ap, t_ap, Bs, n_inv):
            rsq(t_ap, ssq_ap, n_inv, [Bs], [Bs])

        def recip_act(out_ap, in_ap, reads, writes):
            k.op("act", lambda: A.activation(out=out_ap, in_=in_ap, func=AF.Ln), reads=reads, writes=writes)
            k.op("act", lambda: A.activation(out=out_ap, in_=out_ap, func=AF.Exp, scale=-1.0), reads=writes, writes=writes)

        def silu_ps(out_ap, ps_ap, Bps, Bout):
            npart = out_ap.shape[0]
            k.op("act", lambda: A.activation(out=out_ap, in_=ps_ap, func=AF.Exp, scale=-1.0), reads=[Bps], writes=[Bout])
            k.op("act", lambda: A.activation(out=out_ap, in_=out_ap, func=AF.Ln, bias=oneb[:npart, 0:1]), reads=[Bout], writes=[Bout])
            k.op("act", lambda: A.activation(out=out_ap, in_=out_ap, func=AF.Exp, scale=-1.0), reads=[Bout], writes=[Bout])
            k.op("dve", lambda: V.tensor_tensor(out=out_ap, in0=ps_ap, in1=out_ap, op=ALU.mult), reads=[Bps, Bout], writes=[Bout])

        def run_il(gens):
            gens = list(gens)
            while gens:
                nxt = []
                for g in gens:
                    try:
                        next(g)
                        nxt.append(g)
                    except StopIteration:
                        pass
                gens = nxt

        def front(*a, **kw):
            for _ in front_g(*a, **kw):
                pass

        def front_g(src, ntok, g_b, Bg, dstT, BdstT, keep_x=None):
            s = fe["i"] % 2
            fe["i"] += 1
            if keep_x is not None:
                xtile, Bx, dx = keep_x
            else:
                xtile, Bx, dx = xt[s], Bxt[s], dxt[s]
            if src is not None:
                k.dma("sp", xtile[:ntok, :], src, dx, writes=[Bx])
            yield
            k.op("act", lambda: A.activation(out=xn[s][:ntok, :], in_=xtile[:ntok, :], func=AF.Square, accum_out=sm[s][:ntok, 0:1]),
                 reads=[Bx], writes=[Bsm[s], Bxn[s]])
            yield
            rstd_chain(sm[s][:ntok, 0:1], sm[s][:ntok, 1:2], Bsm[s], 1.0 / D)
            yield
            k.op("dve", lambda: V.scalar_tensor_tensor(out=xn[s][:ntok, :], in0=xtile[:ntok, :], scalar=sm[s][:ntok, 1:2], in1=g_b[:ntok, :],
                                                       op0=ALU.mult, op1=ALU.mult), reads=[Bx, Bsm[s], Bg], writes=[Bxn[s]])
            yield
            k.deps("pe", [Bxn[s]], [Bpt[s]])
            for c in range(7 if k.alive() else 0):
                T.transpose(out=pt[s][:, c * 128:c * 128 + ntok], in_=xn[s][:ntok, c * 128:(c + 1) * 128], identity=ident[:ntok, :ntok])
            k.op("pe", lambda: T.transpose(out=pt[s][:, 7 * 128:7 * 128 + ntok], in_=xn[s][:ntok, 7 * 128:8 * 128], identity=ident[:ntok, :ntok]),
                 reads=[Bxn[s]], writes=[Bpt[s]])
            yield
            k.op("act", lambda: A.copy(out=dstT, in_=pt[s][:].rearrange("p (c t) -> p c t", c=8)[:, :, :ntok]), reads=[Bpt[s]], writes=[BdstT])

        def load_w(stack, name, src, kc, ncol, dsem, Bw):
            w = sb(stack, name, [128, kc, ncol], BF16)
            v = src.rearrange("(c p) n -> p c n", p=128)
            for c in range(kc):
                for n0 in range(0, ncol, 1024):
                    n1 = min(ncol, n0 + 1024)
                    k.dma("pool", w[:, c, n0:n1], v[:, c, n0:n1], dsem, writes=[Bw])
            return w

        def bcast_row(stack, name, src_row, n, dsem, Bw):
            t = sb(stack, name, [128, n], F32)
            k.dma("sp", t[:], src_row.partition_broadcast(128), dsem, writes=[Bw])
            return t

        def col_vec(stack, name, src_row, n, dsem, Bw):
            npart = min(n, 128)
            ncol = max(1, n // npart)
            t = sb(stack, name, [128, ncol], F32)
            k.dma("sp", t[:npart, :], src_row.rearrange("o (c p) -> p (o c)", p=npart), dsem, writes=[Bw], allow_slow_non_contiguous=True)
            return t

        dkts = k.dsem("kts"); dvbs = k.dsem("vbs"); Bscr = Buf()
        if "m" in phases:
            with ExitStack() as sm_:
                Bwm = Buf(); dwm = k.dsem("wm"); dwms = k.dsem("wms")
                xnT_m = sb(sm_, "xnT_m", [128, 8, 128], BF16); BxnT_m = Buf()
                kf = sb(sm_, "mkv_kf", [128, 512], F32); Bkf = Buf()
                vf = sb(sm_, "mkv_vf", [128, 512], F32); Bvf = Buf()
                sq = sb(sm_, "mkv_sq", [128, 512], F32); Bsq = Buf()
                s4 = sb(sm_, "mkv_s4", [128, 8], F32); Bs4 = Buf()
                kb = sb(sm_, "mkv_kb", [128, 512], BF16); Bkb = Buf()
                KTm = sb(sm_, "KTm", [128, 4, NMEM], BF16); BKTm = Buf()
                Vbm = sb(sm_, "Vbm", [128, 2, 512], BF16); BVbm = Buf()
                dko, dvo = k.dsem("mkv_ko"), k.dsem("mkv_vo")
                for l in range(2):
                    with ExitStack() as sl:
                        wkv = load_w(sl, "wkv%d" % l, mem_w_kv[l], 8, 1024, dwm, Bwm)
                        gm_b = bcast_row(sl, "gm_b%d" % l, mem_norm_g[l:l + 1, :], D, dwms, Bwm)
                        mkg_b = bcast_row(sl, "mkg_b%d" % l, mem_k_g[l:l + 1, :], 128, dwms, Bwm)
                        for b in range(NB):
                            for mt in range(2):
                                front(mem_p[b, mt * 128:(mt + 1) * 128, :], 128, gm_b, Bwm, xnT_m[:], BxnT_m)
                                for n in range(2):
                                    mmg(pf[n][:], Bpf[n], [(xnT_m[:, c, :], wkv[:, c, n * 512:(n + 1) * 512]) for c in range(8)], [BxnT_m, Bwm])
                                k.op("dve", lambda: V.tensor_copy(out=vf[:], in_=pf[1][:]), reads=[Bpf[1]], writes=[Bvf])
                                k.op("act", lambda: A.copy(out=Vbm[:, mt, :], in_=pf[1][:]), reads=[Bpf[1]], writes=[BVbm])
                                k.dma("sp", nmv_p[l, b, mt * 128:(mt + 1) * 128, :], vf[:], dvo, reads=[Bvf])
                                k.op("act", lambda: A.activation(out=sq[:], in_=pf[0][:], func=AF.Square), reads=[Bpf[0]], writes=[Bsq])
                                k.op("dve", lambda: V.tensor_reduce(out=s4[:, 0:4], in_=sq[:].rearrange("p (h d) -> p h d", h=4), axis=AX.X, op=ALU.add),
                                     reads=[Bsq], writes=[Bs4])
                                rstd_chain(s4[:, 0:4], s4[:, 4:8], Bs4, 1.0 / 128)
                                k.op("dve", lambda: V.tensor_tensor(out=kf[:].rearrange("p (h d) -> p h d", h=4), in0=pf[0][:].rearrange("p (h d) -> p h d", h=4),
                                                                    in1=s4[:, 4:8].unsqueeze(2).to_broadcast([128, 4, 128]), op=ALU.mult),
                                     reads=[Bpf[0], Bs4], writes=[Bkf])
                                k.op("dve", lambda: V.tensor_tensor(out=kf[:].rearrange("p (h d) -> p h d", h=4), in0=kf[:].rearrange("p (h d) -> p h d", h=4),
                                                                    in1=mkg_b[:].unsqueeze(1).to_broadcast([128, 4, 128]), op=ALU.mult),
                                     reads=[Bkf, Bwm], writes=[Bkf])
                                k.op("dve", lambda: V.tensor_copy(out=kb[:], in_=kf[:]), reads=[Bkf], writes=[Bkb])
                                k.dma("sp", nmk_p[l, b, mt * 128:(mt + 1) * 128, :], kf[:], dko, reads=[Bkf])
                                k.deps("pe", [Bkb], [Bpt[0]])
                                for h in range(3 if k.alive() else 0):
                                    T.transpose(out=pt[0][:, h * 128:(h + 1) * 128], in_=kb[:, h * 128:(h + 1) * 128], identity=ident[:])
                                k.op("pe", lambda: T.transpose(out=pt[0][:, 384:512], in_=kb[:, 384:512], identity=ident[:]), reads=[Bkb], writes=[Bpt[0]])
                                k.op("act", lambda: A.copy(out=KTm[:, :, mt * 128:(mt + 1) * 128], in_=pt[0][:, 0:512].rearrange("p (h m) -> p h m", h=4)),
                                     reads=[Bpt[0]], writes=[BKTm])
                            k.dma("sp", kt_d[l, b], KTm[:].rearrange("p h m -> p (h m)"), dkts, reads=[BKTm], writes=[Bscr])
                            k.dma("sp", vb_d[l, b], Vbm[:].rearrange("p a f -> p (a f)"), dvbs, reads=[BVbm], writes=[Bscr])
                        k.barrier()
            k.barrier()

        def mem_q(ntok, xnT, BxnT, w_in, Bw, qoff, l_mqg, qdst, BqT, wk, heads=range(4)):
            sqb, Bsqb, tt, Btt, PT, BPT, sz, Bsz, rd, Brd = wk
            for h in heads:
                c0 = qoff + h * 128
                mmg(pf[0][:, :ntok], Bpf[0], [(w_in[:, c, c0:c0 + 128], xnT[:, c, :ntok]) for c in range(8)], [BxnT, Bw])
                k.op("act", lambda: A.activation(out=sqb[:, :ntok], in_=pf[0][:, :ntok], func=AF.Square), reads=[Bpf[0]], writes=[Bsqb])
                mmg(pf[1][:, :ntok], Bpf[1], [(ones[:], sqb[:, :ntok])], [Bsqb])
                rsq(tt[:, :ntok], pf[1][:, :ntok], 1.0 / 128, [Bpf[1]], [Btt])
                k.op("dve", lambda: V.scalar_tensor_tensor(out=qdst(h), in0=pf[0][:, :ntok], scalar=l_mqg, in1=tt[:, :ntok],
                                                           op0=ALU.mult, op1=ALU.mult), reads=[Bpf[0], Btt, Bw], writes=[BqT])

        def zgate(ntok, xnT, BxnT, w_in, Bw, c0, sz, Bsz):
            mmg(pf[3][:, :ntok], Bpf[3], [(w_in[:, c, c0:c0 + 128], xnT[:, c, :ntok]) for c in range(8)], [BxnT, Bw])
            silu_ps(sz[:, :ntok], pf[3][:, :ntok], Bpf[3], Bsz)

        def mem_core(ntok, q_ap, BqT, KTb, Vbb, BKV, h, wk, sz_ap, Bsz_, out_ap, Bout):
            sqb, Bsqb, tt, Btt, PT, BPT, sz, Bsz, rd, Brd = wk
            for mc in range(2):
                mmg(pf[4 + mc][:, :ntok], Bpf[4 + mc], [(KTb[:, h, mc * 128:(mc + 1) * 128], q_ap)], [BqT] + BKV)
                k.op("act", lambda: A.activation(out=PT[:, mc, :ntok], in_=pf[4 + mc][:, :ntok], func=AF.Exp), reads=[Bpf[4 + mc]], writes=[BPT])
            mmg(pf[4][:, :ntok], Bpf[4], [(Vbb[:, mc, h * 128:(h + 1) * 128], PT[:, mc, :ntok]) for mc in range(2)], [BPT] + BKV)
            mmg(pf[5][:, :ntok], Bpf[5], [(ones[:], PT[:, mc, :ntok]) for mc in range(2)], [BPT])
            recip_act(rd[:, :ntok], pf[5][:, :ntok], [Bpf[5]], [Brd])
            k.op("dve", lambda: V.tensor_tensor(out=rd[:, :ntok], in0=pf[4][:, :ntok], in1=rd[:, :ntok], op=ALU.mult), reads=[Bpf[4], Brd], writes=[Brd])
            k.op("dve", lambda: V.tensor_tensor(out=out_ap, in0=rd[:, :ntok], in1=sz_ap, op=ALU.mult), reads=[Brd, Bsz_], writes=[Bout])

        def out_proj(ntok, outT, BoutT, wout, Bw, xres, Bxres):
            for n in range(2):
                mmg(pf[n][:ntok, :], Bpf[n], [(outT[:, c, :ntok], wout[:, c, n * 512:(n + 1) * 512]) for c in range(12)], [BoutT, Bw])
                k.op("dve", lambda: V.tensor_tensor(out=xres[:ntok, n * 512:(n + 1) * 512], in0=pf[n][:ntok, :], in1=xres[:ntok, n * 512:(n + 1) * 512], op=ALU.add),
                     reads=[Bpf[n], Bxres], writes=[Bxres])

        class NSp:
            pass

        def alloc_blk(stack, W, tag, sample):
            bb = NSp()
            bb.W = W
            bb.xnT = sb(stack, "xnT" + tag, [128, 8, W], BF16); bb.BxnT = Buf()
            bb.outT = sb(stack, "outT" + tag, [128, 12, W], BF16); bb.BoutT = Buf()
            bb.qT = sb(stack, "qT" + tag, [128, 4 if sample else 1, W], BF16); bb.BqT = Buf()
            bb.wk = (sb(stack, "sqb" + tag, [128, W], BF16), Buf(), sb(stack, "tt" + tag, [128, W], F32), Buf(), sb(stack, "PT" + tag, [128, 2, W], BF16), Buf(),
                     sb(stack, "sz" + tag, [128, W], F32), Buf(), sb(stack, "rd" + tag, [128, W], F32), Buf())
            return bb

        dkv = [k.dsem("kvl0"), k.dsem("kvl1")]

        def load_prompt_kv(l, b, KTs, Vbs, BKVs, i):
            s = i % len(KTs)
            k.dma("sp", KTs[s][:].rearrange("p h m -> p (h m)"), kt_d[l, b], dkv[s], reads=[Bscr], writes=[BKVs[s]])
            k.dma("sp", Vbs[s][:].rearrange("p a f -> p (a f)"), vb_d[l, b], dkv[s], reads=[Bscr], writes=[BKVs[s]])
            return KTs[s], Vbs[s], BKVs[s]

        dskv = [k.dsem("skv0"), k.dsem("skv1")]

        def sample_mem_attn(l, ntok_all, qT, BqT, szm, Bszm, outT, BoutT, wk, stack):
            ks = [sb(stack, "sks%d_%d" % (l, i), [128, 2, 512], BF16) for i in range(2)]
            vs = [sb(stack, "svs%d_%d" % (l, i), [128, 2, 512], BF16) for i in range(2)]
            Bks = [Buf(), Buf()]
            KTs = [sb(stack, "sKT%d_%d" % (l, i), [128, 4, NMEM], BF16) for i in range(1)] * 2; BKTs = [Buf()] * 2
            for b in range(BS):
                s = b % 2
                k.dma("pool", ks[s][:], c_mk[l, b].rearrange("(mt p) f -> p mt f", p=128), dskv[s], writes=[Bks[s]])
                k.dma("pool", vs[s][:], c_mv[l, b].rearrange("(mt p) f -> p mt f", p=128), dskv[s], writes=[Bks[s]])
                k.deps("pe", [Bks[s]], [Bpt[s]])
                lst = [(mt, h) for mt in range(2) for h in range(4)]
                for (mt, h) in (lst[:-1] if k.alive() else []):
                    T.transpose(out=pt[s][:, h * 256 + mt * 128:h * 256 + (mt + 1) * 128], in_=ks[s][:, mt, h * 128:(h + 1) * 128], identity=ident[:])
                k.op("pe", lambda: T.transpose(out=pt[s][:, 3 * 256 + 128:4 * 256], in_=ks[s][:, 1, 384:512], identity=ident[:]), reads=[Bks[s]], writes=[Bpt[s]])
                k.op("act", lambda: A.copy(out=KTs[s][:].rearrange("p h m -> p (h m)"), in_=pt[s][:]), reads=[Bpt[s]], writes=[BKTs[s]])
                for h in range(4):
                    mem_core(TS, qT[:, h, b * TS:(b + 1) * TS], BqT, KTs[s], vs[s], [Bks[s], BKTs[s]], h, wk, szm[:, h, b * TS:(b + 1) * TS], Bszm,
                             outT[:, 8 + h, b * TS:(b + 1) * TS], BoutT)
                k.deps("pe", [BKTs[s]], [])

        with ExitStack() as s0:
            Bw0 = Buf()
            dw0 = k.dsem("w0"); dw0s = k.dsem("w0s")
            win0 = load_w(s0, "win0", conv_w_in, 8, 5120, dw0, Bw0)
            wout0 = load_w(s0, "wout0", conv_w_out, 12, 1024, dw0, Bw0)
            g0_b = bcast_row(s0, "g0_b", conv_norm_g[0:1, :], D, dw0s, Bw0)
            mqg0 = col_vec(s0, "mqg0", mem_q_g[0:1, :], 128, dw0s, Bw0)
            cw = sb(s0, "cw", [128, 8, 3], F32)
            for j in range(3):
                k.dma("sp", cw[:, :, j], conv_w[j:j + 1, :].rearrange("o (c p) -> p (o c)", p=128), dw0s, writes=[Bw0], allow_slow_non_contiguous=True)
            k.op("dve", lambda: V.tensor_scalar(out=mqg0[:, 0:1], in0=mqg0[:, 0:1], scalar1=MEM_SCALE, scalar2=None, op0=ALU.mult), reads=[Bw0], writes=[Bw0])
            KTs0 = [sb(s0, "KT0_0", [128, 4, NMEM], BF16)]
            Vbs0 = [sb(s0, "Vb0_0", [128, 2, 512], BF16)]
            BKVs0 = [Buf()]
            dyo = [k.dsem("yo0"), k.dsem("yo1")]
            ncp = sb(s0, "ncp", [128, 8, 2 * max(1, BS)], F32); Bncp = Buf(); dncp = k.dsem("ncp")

            def alloc_l0(stack, W, tag, sample):
                bb = alloc_blk(stack, W, tag, sample)
                bb.gu = sb(stack, "gu" + tag, [128, 8, max(W + 2, BS * (TS + 2)) if sample else W + 2], F32); bb.Bgu = [Buf() for _ in range(8)]
                bb.tcv = sb(stack, "tcv" + tag, [128, W], F32); bb.Btcv = Buf()
                bb.szc = sb(stack, "szc" + tag, [128, W], F32); bb.Bszc = Buf()
                return bb

            def conv_chunks(bb, ntok, nb, tl, zero_hist):
                xnT, BxnT, outT, BoutT, gu, Bgu, tcv, Btcv, szc, Bszc = bb.xnT, bb.BxnT, bb.outT, bb.BoutT, bb.gu, bb.Bgu, bb.tcv, bb.Btcv, bb.szc, bb.Bszc
                v3 = lambda ap: ap.rearrange("p (b t) -> p b t", b=nb)
                for c in range(8):
                    guc = gu[:, c, :nb * (tl + 2)].rearrange("p (b t) -> p b t", b=nb)
                    gbk = 2 if c % 2 == 0 else 4
                    zbk = 3 if c % 2 == 0 else 5
                    for j, off in ((0, 0), (1, 2048), (gbk, 1024), (zbk, 3584)):
                        mmg(pf[j][:, :ntok], Bpf[j], [(win0[:, kc, off + c * 128:off + (c + 1) * 128], xnT[:, kc, :ntok]) for kc in range(8)], [BxnT, Bw0])
                    k.op("act", lambda: A.copy(out=guc[:, :, 2:tl + 2], in_=v3(pf[0][:, :ntok])), reads=[Bpf[0]], writes=[Bgu[c]])
                    silu_ps(szc[:, :ntok], pf[zbk][:, :ntok], Bpf[zbk], Bszc)
                    if zero_hist:
                        k.op("dve", lambda: V.memset(guc[:, :, 0:2], 0.0), reads=[Bgu[c]], writes=[Bgu[c]])
                    k.op("dve", lambda: V.tensor_tensor(out=guc[:, :, 2:tl + 2], in0=v3(pf[1][:, :ntok]), in1=guc[:, :, 2:tl + 2], op=ALU.mult),
                         reads=[Bpf[1], Bgu[c]], writes=[Bgu[c]])
                    k.op("dve", lambda: V.tensor_scalar(out=v3(tcv[:, :ntok]), in0=guc[:, :, 0:tl], scalar1=cw[:, c, 0:1], scalar2=None, op0=ALU.mult),
                         reads=[Bgu[c], Bw0], writes=[Btcv])
                    for j in (1, 2):
                        k.op("dve", lambda: V.scalar_tensor_tensor(out=v3(tcv[:, :ntok]), in0=guc[:, :, j:tl + j], scalar=cw[:, c, j:j + 1], in1=v3(tcv[:, :ntok]),
                                                                   op0=ALU.mult, op1=ALU.add), reads=[Bgu[c], Btcv, Bw0], writes=[Btcv])
                    k.op("dve", lambda: V.tensor_tensor(out=tcv[:, :ntok], in0=pf[gbk][:, :ntok], in1=tcv[:, :ntok], op=ALU.mult), reads=[Bpf[gbk], Btcv], writes=[Btcv])
                    k.op("dve", lambda: V.tensor_tensor(out=outT[:, c, :ntok], in0=tcv[:, :ntok], in1=szc[:, :ntok], op=ALU.mult), reads=[Btcv, Bszc], writes=[BoutT])
                    k.op("dve", lambda: V.tensor_copy(out=guc[:, :, 0:2], in_=guc[:, :, tl:tl + 2]), reads=[Bgu[c]], writes=[Bgu[c]])

            if "l0p" in phases:
              with ExitStack() as sp0:
                bb = alloc_l0(sp0, TB, "0p", False)
                xnT, BxnT, outT, BoutT, wk = bb.xnT, bb.BxnT, bb.outT, bb.BoutT, bb.wk
                for b in range(NB):
                    KTb, Vbb, BKVb = load_prompt_kv(0, b, KTs0, Vbs0, BKVs0, b)
                    for ib in range(NBLK):
                        t0 = ib * TB
                        for tp_ in range(2):
                            run_il([front_g(x_p[b, t0 + ti * 128:t0 + (ti + 1) * 128, :], 128, g0_b, Bw0, xnT[:, :, ti * 128:(ti + 1) * 128], BxnT)
                                    for ti in (2 * tp_, 2 * tp_ + 1)])
                        conv_chunks(bb, TB, 1, TB, ib == 0)
                        for h in range(4):
                            mem_q(TB, xnT, BxnT, win0, Bw0, 3072, mqg0[:, 0:1], lambda hh: bb.qT[:, 0, :], bb.BqT, wk, heads=[h])
                            zgate(TB, xnT, BxnT, win0, Bw0, 4608 + h * 128, wk[6], wk[7])
                            mem_core(TB, bb.qT[:, 0, :], bb.BqT, KTb, Vbb, [BKVb], h, wk, wk[6][:, :TB], wk[7], outT[:, 8 + h, :], BoutT)
                        for ti in range(4):
                            s = ti % 2
                            k.dma("sp", xt[s][:], x_p[b, t0 + ti * 128:t0 + (ti + 1) * 128, :], dxt[s], writes=[Bxt[s]])
                            out_proj(128, outT[:, :, ti * 128:(ti + 1) * 128], BoutT, wout0, Bw0, xt[s], Bxt[s])
                            ydst = y1_d if "l1p" in phases else y_p
                            k.dma("sp", ydst[b, t0 + ti * 128:t0 + (ti + 1) * 128, :], xt[s][:], dyo[s], reads=[Bxt[s]], writes=[Bscr])
                    k.op("dve", lambda: V.tensor_copy(out=ncp[:, :, 0:2], in_=bb.gu[:, :, 0:2]), reads=bb.Bgu, writes=[Bncp])
                    for c in range(8):
                        k.dma("sp", nconv_p[2 * b:2 * b + 2, c * 128:(c + 1) * 128].rearrange("j p -> p j"), ncp[:, c, 0:2], dncp, reads=[Bncp], allow_slow_non_contiguous=True)
                k.barrier()

            if "l0s" in phases:
                with ExitStack() as ss:
                    bb = alloc_l0(ss, NS, "0s", True)
                    xnT, BxnT, outT, BoutT, wk, gu, Bgu = bb.xnT, bb.BxnT, bb.outT, bb.BoutT, bb.wk, bb.gu, bb.Bgu
                    hs = sb(ss, "hs", [2 * BS, D], F32); Bhs = Buf(); dhs = k.dsem("hs")
                    szm = sb(ss, "szm0", [128, 4, NS], F32); Bszm = Buf()
                    k.dma("sp", hs[:], st_conv[:, :], dhs, writes=[Bhs])
                    front(x_s[:, :], NS, g0_b, Bw0, xnT[:, :, :NS], BxnT, keep_x=(y1s, By1s, dy1s))
                    k.deps("pe", [Bhs], [Bpf[4]])
                    for c in range(7 if k.alive() else 0):
                        T.transpose(out=pf[4][:, c * 2 * BS:(c + 1) * 2 * BS], in_=hs[:, c * 128:(c + 1) * 128], identity=identf[:2 * BS, :2 * BS])
                    k.op("pe", lambda: T.transpose(out=pf[4][:, 7 * 2 * BS:8 * 2 * BS], in_=hs[:, 7 * 128:8 * 128], identity=identf[:2 * BS, :2 * BS]),
                         reads=[Bhs], writes=[Bpf[4]])
                    for c in range(8):
                        guc = gu[:, c, :BS * (TS + 2)].rearrange("p (b t) -> p b t", b=BS)
                        k.op("dve", lambda: V.tensor_copy(out=guc[:, :, 0:2], in_=pf[4][:, c * 2 * BS:(c + 1) * 2 * BS].rearrange("p (b j) -> p b j", b=BS)),
                             reads=[Bpf[4]], writes=[Bgu[c]])
                    conv_chunks(bb, NS, BS, TS, False)
                    for c in range(8):
                        guc = gu[:, c, :BS * (TS + 2)].rearrange("p (b t) -> p b t", b=BS)
                        k.op("dve", lambda: V.tensor_copy(out=ncp[:, c, :2 * BS].rearrange("p (b j) -> p b j", b=BS), in_=guc[:, :, 0:2]), reads=[Bgu[c]], writes=[Bncp])
                    for half in range(2):
                        k.deps("pe", [Bncp], [Bpf[4 + half]])
                        for cc in range(3 if k.alive() else 0):
                            c = half * 4 + cc
                            T.transpose(out=pf[4 + half][:2 * BS, cc * 128:(cc + 1) * 128], in_=ncp[:, c, :2 * BS], identity=identf[:, :])
                        c = half * 4 + 3
                        k.op("pe", lambda: T.transpose(out=pf[4 + half][:2 * BS, 384:512], in_=ncp[:, c, :2 * BS], identity=identf[:, :]), reads=[Bncp], writes=[Bpf[4 + half]])
                        k.op("dve", lambda: V.tensor_copy(out=hs[:, half * 512:(half + 1) * 512], in_=pf[4 + half][:2 * BS, :]), reads=[Bpf[4 + half]], writes=[Bhs])
                    k.dma("sp", nconv_s[:, :], hs[:], dhs, reads=[Bhs])
                    mem_q(NS, xnT, BxnT, win0, Bw0, 3072, mqg0[:, 0:1], lambda hh: bb.qT[:, hh, :NS], bb.BqT, wk)
                    for h in range(4):
                        zgate(NS, xnT, BxnT, win0, Bw0, 4608 + h * 128, szm[:, h, :], Bszm)
                    sample_mem_attn(0, NS, bb.qT, bb.BqT, szm, Bszm, outT, BoutT, wk, ss)
                    out_proj(NS, outT, BoutT, wout0, Bw0, y1s, By1s)
                    if "l1s" not in phases:
                        k.dma("sp", y_s[:, :], y1s[:NS, :], dy1s, reads=[By1s])
                    k.barrier()
            k.barrier()

        if "l1p" in phases or "l1s" in phases:
          with ExitStack() as s1:
            Bw1 = Buf(); dw1 = k.dsem("w1"); dw1s = k.dsem("w1s")
            win1 = load_w(s1, "win1", mla_w_in, 8, 2752, dw1, Bw1)
            wuq = load_w(s1, "wuq", mla_w_uq, 3, 1536, dw1, Bw1)
            wuk = load_w(s1, "wuk", mla_w_uk, 2, 1024, dw1, Bw1)
            wuv = load_w(s1, "wuv", mla_w_uv, 2, 1024, dw1, Bw1)
            wout1 = load_w(s1, "wout1", mla_w_out, 12, 1024, dw1, Bw1)
            wukT = sb(s1, "wukT", [128, NH, KVL], BF16)
            g1_b = bcast_row(s1, "g1_b", mla_norm_g[0:1, :], D, dw1s, Bw1)
            ckvg_b = bcast_row(s1, "ckvg_b", mla_ckv_g[0:1, :], KVL, dw1s, Bw1)
            kgr_b = bcast_row(s1, "kgr_b", mla_k_g[0:1, NOPE:QKH], ROPE, dw1s, Bw1)
            mqg1 = col_vec(s1, "mqg1", mem_q_g[1:2, :], 128, dw1s, Bw1)
            qlg = col_vec(s1, "qlg", mla_q_lora_g[0:1, :], QL, dw1s, Bw1)
            qgn = col_vec(s1, "qgn", mla_q_g[0:1, 0:NOPE], NOPE, dw1s, Bw1)
            kgn = col_vec(s1, "kgn", mla_k_g[0:1, 0:NOPE], NOPE, dw1s, Bw1)
            qgr = col_vec(s1, "qgr", mla_q_g[0:1, NOPE:QKH], ROPE, dw1s, Bw1)
            k.op("dve", lambda: V.tensor_scalar(out=mqg1[:, 0:1], in0=mqg1[:, 0:1], scalar1=MEM_SCALE, scalar2=None, op0=ALU.mult), reads=[Bw1], writes=[Bw1])
            k.op("dve", lambda: V.tensor_tensor(out=qgn[:, 0:1], in0=qgn[:, 0:1], in1=kgn[:, 0:1], op=ALU.mult), reads=[Bw1], writes=[Bw1])
            for half in range(2):
                k.deps("pe", [Bw1], [Bpt[half]])
                lst = [(h, cc) for h in range(half * 4, half * 4 + 4) for cc in range(2)]
                for (h, cc) in (lst[:-1] if k.alive() else []):
                    T.transpose(out=pt[half][:, (h % 4) * 256 + cc * 128:(h % 4) * 256 + (cc + 1) * 128], in_=wuk[:, cc, h * 128:(h + 1) * 128], identity=ident[:])
                h, cc = lst[-1]
                k.op("pe", lambda: T.transpose(out=pt[half][:, (h % 4) * 256 + cc * 128:(h % 4) * 256 + (cc + 1) * 128], in_=wuk[:, cc, h * 128:(h + 1) * 128], identity=ident[:]),
                     reads=[Bw1], writes=[Bpt[half]])
                k.op("act", lambda: A.copy(out=wukT[:, half * 4:half * 4 + 4, :].rearrange("p h c -> p (h c)"), in_=pt[half][:]), reads=[Bpt[half]], writes=[Bw1])

            KTs1 = [sb(s1, "KT1_0", [128, 4, NMEM], BF16)]
            Vbs1 = [sb(s1, "Vb1_0", [128, 2, 512], BF16)]
            BKVs1 = [Buf()]
            Btab = Buf(); dtab = k.dsem("tab")

            def alloc_l1(stack, W, tag, sample):
                bb = alloc_blk(stack, W, tag, sample)
                f = lambda n, shp, dt: sb(stack, n + tag, shp, dt)
                return (bb, bb.xnT, bb.BxnT, bb.outT, bb.BoutT, bb.wk) + bb.wk + (
                    f("cq_sb", [128, 3, W], F32), Buf(), f("cqn", [128, 3, W], BF16), Buf(), f("qn_sb", [128, 2], F32), Buf(),
                    f("qr_sb", [64, 2], F32), Buf(), f("sqr", [64, W], BF16), Buf(), f("qnope", [128, W], BF16), Buf(),
                    f("qrb", [64, W], BF16), Buf(), f("ta", [64, W], F32), Buf(), f("tb", [64, W], F32), Buf(),
                    f("qabs", [128, 2, W], BF16), Buf(), f("qpe", [64, W], BF16), Buf(), f("olat", [128, 2, W], BF16), Buf(),
                    f("cosT", [64, W], F32), f("sinT", [64, W], F32))

            (bb, xnT, BxnT, outT, BoutT, wk, sqb, Bsqb, tt, Btt, PT, BPT, sz, Bsz, rd, Brd, cq_sb, Bcq, cqn, Bcqn, qn_sb, Bqn, qr_sb, Bqr, sqr, Bsqr,
             qnope, Bqnope, qrb, Bqrb, ta, Bta, tb_, Btb, qabs, Bqabs, qpe, Bqpe, olat, Bolat, cosT, sinT) = (None,) * 42
            tmj = sb(s1, "tmj", [128, 512], BF16)

            class TW:
                pass
            tws = []
            for i_ in range(2):
                w_ = TW()
                sfx = "_%d" % i_
                w_.i = i_
                w_.tm = sb(s1, "tm" + sfx, [128, 32], F32); w_.Btm = Buf()
                w_.ckvf = sb(s1, "ckvf" + sfx, [128, KVL], F32); w_.Bckvf = Buf()
                w_.kr = sb(s1, "kr" + sfx, [128, 64], F32); w_.Bkr = Buf()
                w_.t1 = sb(s1, "t1" + sfx, [128, 64], F32); w_.Bt1 = Buf()
                w_.t2 = sb(s1, "t2" + sfx, [128, 64], F32); w_.Bt2 = Buf()
                w_.kpef = sb(s1, "kpef" + sfx, [128, 64], F32); w_.Bkpef = Buf()
                w_.kpeb = sb(s1, "kpeb" + sfx, [128, 64], BF16); w_.Bkpeb = Buf()
                w_.csA = sb(s1, "csA" + sfx, [128, 64], F32); w_.csB = sb(s1, "csB" + sfx, [128, 64], F32); w_.Bcs = Buf(); w_.dcs = k.dsem("cs" + sfx)
                w_.sqk = sb(s1, "sqk" + sfx, [128, 512], F32); w_.Bsqk = Buf()
                w_.kscf = sb(s1, "kscf" + sfx, [128, 8], F32); w_.Bkscf = Buf()
                w_.dto = [k.dsem("to0" + sfx), k.dsem("to1" + sfx), k.dsem("to2" + sfx)]
                tws.append(w_)
            dyo1 = [k.dsem("y1o0"), k.dsem("y1o1")]

            def tm_tile(*a, **kw):
                for _ in tm_tile_g(*a, **kw):
                    pass

            def tm_tile_g(w, ntok, xcols, csA_src, csB_src, ckv_dst, kpe_dst, ksc_dst, ckvT_dst, kpeT_dst, ckvtok_dst, kss_dst, Bseq):
                tm, Btm, ckvf, Bckvf, kr, Bkr, t1, Bt1, t2, Bt2 = w.tm, w.Btm, w.ckvf, w.Bckvf, w.kr, w.Bkr, w.t1, w.Bt1, w.t2, w.Bt2
                kpef, Bkpef, kpeb, Bkpeb, csA, csB, Bcs, dcs, sqk, Bsqk, kscf, Bkscf, dto = (w.kpef, w.Bkpef, w.kpeb, w.Bkpeb, w.csA, w.csB, w.Bcs, w.dcs,
                                                                                            w.sqk, w.Bsqk, w.kscf, w.Bkscf, w.dto)
                pb = w.i
                kb0 = 2 + 2 * w.i
                mmg(pf[pb][:ntok, 0:320], Bpf[pb], [(xcols(c), win1[:, c, QL:QL + 320]) for c in range(8)], [BxnT, Bw1])
                k.dma("sp", csA[:ntok, :], csA_src, dcs, writes=[Bcs])
                k.dma("sp", csB[:ntok, :], csB_src, dcs, writes=[Bcs])
                yield
                k.op("act", lambda: A.activation(out=tmj[:ntok, 0:256], in_=pf[pb][:ntok, 0:256], func=AF.Square, accum_out=tm[:ntok, 0:1]), reads=[Bpf[pb]], writes=[Btm])
                k.op("act", lambda: A.activation(out=tmj[:ntok, 256:320], in_=pf[pb][:ntok, 256:320], func=AF.Square, accum_out=tm[:ntok, 1:2]), reads=[Bpf[pb]], writes=[Btm])
                yield
                rstd_chain(tm[:ntok, 0:1], tm[:ntok, 2:3], Btm, 1.0 / KVL)
                yield
                k.op("dve", lambda: V.scalar_tensor_tensor(out=ckvf[:ntok, :], in0=pf[pb][:ntok, 0:256], scalar=tm[:ntok, 2:3], in1=ckvg_b[:ntok, :], op0=ALU.mult, op1=ALU.mult),
                     reads=[Bpf[pb], Btm, Bw1], writes=[Bckvf])
                k.dma("sp", ckv_dst, ckvf[:ntok, :], dto[0], reads=[Bckvf])
                k.op("dve", lambda: V.tensor_tensor(out=kr[:ntok, :], in0=pf[pb][:ntok, 256:320], in1=kgr_b[:ntok, :], op=ALU.mult), reads=[Bpf[pb], Bw1], writes=[Bkr])
                yield
                k.op("dve", lambda: V.tensor_copy(out=ckvtok_dst, in_=ckvf[:ntok, :]), reads=[Bckvf], writes=[Bseq])
                k.op("dve", lambda: V.tensor_tensor(out=t1[:ntok, :], in0=kr[:ntok, :], in1=csA[:ntok, :], op=ALU.mult), reads=[Bkr, Bcs], writes=[Bt1])
                k.op("dve", lambda: V.tensor_tensor(out=t2[:ntok, 0:32], in0=kr[:ntok, 32:64], in1=csB[:ntok, 0:32], op=ALU.mult), reads=[Bkr, Bcs], writes=[Bt2])
                k.op("dve", lambda: V.tensor_tensor(out=t2[:ntok, 32:64], in0=kr[:ntok, 0:32], in1=csB[:ntok, 32:64], op=ALU.mult), reads=[Bkr, Bcs], writes=[Bt2])
                yield
                k.op("dve", lambda: V.tensor_tensor(out=kpef[:ntok, :], in0=t1[:ntok, :], in1=t2[:ntok, :], op=ALU.add), reads=[Bt1, Bt2], writes=[Bkpef])
                k.dma("sp", kpe_dst, kpef[:ntok, :], dto[1], reads=[Bkpef])
                yield
                k.op("dve", lambda: V.tensor_copy(out=kpeb[:ntok, :], in_=kpef[:ntok, :]), reads=[Bkpef], writes=[Bkpeb])
                yield
                j = w.i
                k.deps("pe", [Bseq, Bkpeb], [Bpt[j]])
                if k.alive():
                    T.transpose(out=pt[j][:, 0:ntok], in_=ckvtok_dst[:, 0:128], identity=ident[:ntok, :ntok])
                    T.transpose(out=pt[j][:, 128:128 + ntok], in_=ckvtok_dst[:, 128:256], identity=ident[:ntok, :ntok])
                k.op("pe", lambda: T.transpose(out=pt[j][0:64, 256:256 + ntok], in_=kpeb[:ntok, :], identity=ident[:ntok, :ntok]), reads=[Bseq, Bkpeb], writes=[Bpt[j]])
                yield
                k.op("act", lambda: A.copy(out=ckvT_dst, in_=pt[j][:, 0:256].rearrange("p (c t) -> p c t", c=2)[:, :, :ntok]), reads=[Bpt[j]], writes=[Bseq])
                k.op("act", lambda: A.copy(out=kpeT_dst, in_=pt[j][0:64, 256:256 + ntok]), reads=[Bpt[j]], writes=[Bseq])
                yield
                for n in range(2):
                    mmg(pf[kb0 + n][:ntok, :], Bpf[kb0 + n], [(ckvT_dst[:, cc, :], wuk[:, cc, n * 512:(n + 1) * 512]) for cc in range(2)], [Bseq, Bw1])
                yield
                for n in range(2):
                    k.op("act", lambda: A.activation(out=sqk[:ntok, :], in_=pf[kb0 + n][:ntok, :], func=AF.Square), reads=[Bpf[kb0 + n]], writes=[Bsqk])
                    yield
                    k.op("dve", lambda: V.tensor_reduce(out=tm[:ntok, 4 + 4 * n:8 + 4 * n], in_=sqk[:ntok, :].rearrange("p (h d) -> p h d", h=4), axis=AX.X, op=ALU.add),
                         reads=[Bsqk], writes=[Btm])
                    yield
                k.op("dve", lambda: V.tensor_scalar(out=tm[:ntok, 3:4], in0=tm[:ntok, 1:2], scalar1=1.0 / QKH, scalar2=EPS, op0=ALU.mult, op1=ALU.add), reads=[Btm], writes=[Btm])
                yield
                rsq(kscf[:ntok, :], tm[:ntok, 4:12], 1.0 / QKH, [Btm], [Bkscf], bias_ap=tm[:ntok, 3:4])
                yield
                k.dma("sp", ksc_dst, kscf[:ntok, :], dto[2], reads=[Bkscf])
                k.op("dve", lambda: V.tensor_scalar(out=kss_dst, in0=kscf[:ntok, :], scalar1=SM_SCALE, scalar2=None, op0=ALU.mult), reads=[Bkscf], writes=[Bseq])

            def cq_part(ntok):
                for c in range(3):
                    mmg(pf[c][:, :ntok], Bpf[c], [(win1[:, kc, c * 128:(c + 1) * 128], xnT[:, kc, :ntok]) for kc in range(8)], [BxnT, Bw1])
                    k.op("dve", lambda: V.tensor_scalar(out=cq_sb[:, c, :ntok], in0=pf[c][:, :ntok], scalar1=qlg[:, c:c + 1], scalar2=None, op0=ALU.mult),
                         reads=[Bpf[c], Bw1], writes=[Bcq])
                    k.op("act", lambda: A.activation(out=PT[:, 0, :ntok] if c % 2 == 0 else PT[:, 1, :ntok], in_=pf[c][:, :ntok], func=AF.Square), reads=[Bpf[c]], writes=[BPT])
                    mmg(pf[3][:, :ntok], Bpf[3], [(ones[:], PT[:, c % 2, :ntok])], [BPT], start=(c == 0), stop=(c == 2))
                rsq(tt[:, :ntok], pf[3][:, :ntok], 1.0 / QL, [Bpf[3]], [Btt])
                for c in range(3):
                    k.op("dve", lambda: V.tensor_tensor(out=cqn[:, c, :ntok], in0=cq_sb[:, c, :ntok], in1=tt[:, :ntok], op=ALU.mult), reads=[Bcq, Btt], writes=[Bcqn])

            def q_head(h, ntok, cos_ap, sin_ap, vw, qabs_dst, qpe_dst, Bqa, Bqp):
                mmg(pf[0][:, :ntok], Bpf[0], [(wuq[:, c, h * QKH:h * QKH + NOPE], cqn[:, c, :ntok]) for c in range(3)], [Bcqn, Bw1])
                mmg(pf[1][:64, :ntok], Bpf[1], [(wuq[:, c, h * QKH + NOPE:(h + 1) * QKH], cqn[:, c, :ntok]) for c in range(3)], [Bcqn, Bw1])
                k.op("act", lambda: A.activation(out=sqb[:, :ntok], in_=pf[0][:, :ntok], func=AF.Square), reads=[Bpf[0]], writes=[Bsqb])
                k.op("act", lambda: A.activation(out=sqr[:, :ntok], in_=pf[1][:64, :ntok], func=AF.Square), reads=[Bpf[1]], writes=[Bsqr])
                mmg(pf[2][:, :ntok], Bpf[2], [(ones[:, :], sqb[:, :ntok]), (ones[0:64, :], sqr[:, :ntok])], [Bsqb, Bsqr])
                rsq(tt[:, :ntok], pf[2][:, :ntok], 1.0 / QKH, [Bpf[2]], [Btt])
                k.op("dve", lambda: V.scalar_tensor_tensor(out=qnope[:, :ntok], in0=pf[0][:, :ntok], scalar=qgn[:, 0:1], in1=tt[:, :ntok], op0=ALU.mult, op1=ALU.mult),
                     reads=[Bpf[0], Btt, Bw1], writes=[Bqnope])
                k.op("dve", lambda: V.scalar_tensor_tensor(out=qrb[:, :ntok], in0=pf[1][:64, :ntok], scalar=qgr[:64, 0:1], in1=tt[:64, :ntok], op0=ALU.mult, op1=ALU.mult),
                     reads=[Bpf[1], Btt, Bw1], writes=[Bqrb])
                mmg(pf[3][:64, :ntok], Bpf[3], [(rot[:, :], qrb[:, :ntok])], [Bqrb])
                k.op("dve", lambda: V.tensor_tensor(out=ta[:, :ntok], in0=qrb[:, :ntok], in1=cos_ap, op=ALU.mult), reads=[Bqrb, Btab], writes=[Bta])
                k.op("dve", lambda: V.tensor_tensor(out=tb_[:, :ntok], in0=pf[3][:64, :ntok], in1=sin_ap, op=ALU.mult), reads=[Bpf[3], Btab], writes=[Btb])
                k.op("dve", lambda: V.tensor_tensor(out=qpe_dst, in0=vw(ta[:, :ntok]), in1=vw(tb_[:, :ntok]), op=ALU.add), reads=[Bta, Btb], writes=[Bqp])
                for cc in range(2):
                    mmg(pf[4 + cc][:, :ntok], Bpf[4 + cc], [(wukT[:, h, cc * 128:(cc + 1) * 128], qnope[:, :ntok])], [Bqnope, Bw1])
                    if cc == 0:
                        k.op("act", lambda: A.copy(out=qabs_dst(cc), in_=vw(pf[4 + cc][:, :ntok])), reads=[Bpf[4 + cc]], writes=[Bqa])
                    else:
                        k.op("dve", lambda: V.tensor_copy(out=qabs_dst(cc), in_=vw(pf[4 + cc][:, :ntok])), reads=[Bpf[4 + cc]], writes=[Bqa])

            def head_out(h, ntok, olat_rhs, Bol):
                mmg(pf[5][:, :ntok], Bpf[5], [(wuv[:, cc, h * 128:(h + 1) * 128], olat_rhs(cc)) for cc in range(2)], [Bol, Bw1])
                zgate(ntok, xnT, BxnT, win1, Bw1, 1216 + h * 128, sz, Bsz)
                k.op("dve", lambda: V.tensor_tensor(out=outT[:, h, :ntok], in0=pf[5][:, :ntok], in1=sz[:, :ntok], op=ALU.mult), reads=[Bpf[5], Bsz], writes=[BoutT])

            if "l1p" in phases:
              with ExitStack() as sp1:
                (bb, xnT, BxnT, outT, BoutT, wk, sqb, Bsqb, tt, Btt, PT, BPT, sz, Bsz, rd, Brd, cq_sb, Bcq, cqn, Bcqn, qn_sb, Bqn, qr_sb, Bqr, sqr, Bsqr,
                 qnope, Bqnope, qrb, Bqrb, ta, Bta, tb_, Btb, qabs, Bqabs, qpe, Bqpe, olat, Bolat, cosT, sinT) = alloc_l1(sp1, TB, "1p", False)
                ckvT = sb(sp1, "ckvT", [128, 2, S], BF16); kpeT = sb(sp1, "kpeT", [64, S], BF16)
                ckvtok = sb(sp1, "ckvtok", [128, NKT, KVL], BF16); kss = sb(sp1, "kss", [128, NKT, NH], F32); Bseq = Buf()
                PTa = [PT[:, 0, :], PT[:, 1, :]]; BPTa = [Buf(), Buf()]
                for b in range(NB):
                    KTb, Vbb, BKVb = load_prompt_kv(1, b, KTs1, Vbs1, BKVs1, b)
                    for ib in range(NBLK):
                        t0 = ib * TB
                        k.dma("sp", cosT[:, :], c_cosT_p[:, t0:t0 + TB], dtab, writes=[Btab])
                        k.dma("sp", sinT[:, :], c_sinT_p[:, t0:t0 + TB], dtab, writes=[Btab])
                        for tp_ in range(2):
                            run_il([front_g(y1_d[b, t0 + ti * 128:t0 + (ti + 1) * 128, :], 128, g1_b, Bw1, xnT[:, :, ti * 128:(ti + 1) * 128], BxnT)
                                    for ti in (2 * tp_, 2 * tp_ + 1)])
                        def tmg(ti, w):
                            tk0 = t0 + ti * 128
                            kt = tk0 // 128
                            return tm_tile_g(w, 128, lambda c: xnT[:, c, ti * 128:(ti + 1) * 128], c_csA_p[tk0:tk0 + 128, :], c_csB_p[tk0:tk0 + 128, :],
                                             ckv_p[b, tk0:tk0 + 128, :], kpe_p[b, tk0:tk0 + 128, :], ksc_p[b, tk0:tk0 + 128, :],
                                             ckvT[:, :, tk0:tk0 + 128], kpeT[:, tk0:tk0 + 128], ckvtok[:, kt, :], kss[:, kt, :], Bseq)
                        for tp_ in range(2):
                            run_il([tmg(2 * tp_, tws[0]), tmg(2 * tp_ + 1, tws[1])])
                        cq_part(TB)
                        nkt = (t0 + TB) // 128
                        for h in range(NH):
                            q_head(h, TB, cosT[:, :], sinT[:, :], lambda a: a, lambda cc: qabs[:, cc, :], qpe[:, :], Bqabs, Bqpe)

                            def S_(kt):
                                col0 = max(0, kt * 128 - t0)
                                bank = 3 + kt % 2
                                mmg(pf[bank][:, col0:TB], Bpf[bank], [(ckvT[:, 0, kt * 128:(kt + 1) * 128], qabs[:, 0, col0:TB]),
                                                                       (ckvT[:, 1, kt * 128:(kt + 1) * 128], qabs[:, 1, col0:TB]),
                                                                       (kpeT[:, kt * 128:(kt + 1) * 128], qpe[:, col0:TB])], [Bseq, Bqabs, Bqpe])

                            def E_(kt):
                                col0 = max(0, kt * 128 - t0)
                                bank = 3 + kt % 2
                                k.op("act", lambda: A.activation(out=PTa[kt % 2][:, col0:TB], in_=pf[bank][:, col0:TB], func=AF.Exp, scale=kss[:, kt, h:h + 1]),
                                     reads=[Bpf[bank], Bseq], writes=[BPTa[kt % 2]])
                                if kt * 128 >= t0:
                                    k.op("dve", lambda: V.tensor_tensor(out=PTa[kt % 2][:, col0:col0 + 128], in0=PTa[kt % 2][:, col0:col0 + 128], in1=mask[:, :], op=ALU.mult),
                                         reads=[BPTa[kt % 2]], writes=[BPTa[kt % 2]])

                            def PV_(kt):
                                col0 = max(0, kt * 128 - t0)
                                if not k.alive():
                                    return
                                k.deps("pe", [BPTa[kt % 2], Bseq], [Bpf[0], Bpf[1], Bpf[2]])
                                T.matmul(pf[0][:, col0:TB], lhsT=ckvtok[:, kt, 0:128], rhs=PTa[kt % 2][:, col0:TB], start=(kt == 0), stop=(kt == nkt - 1))
                                T.matmul(pf[1][:, col0:TB], lhsT=ckvtok[:, kt, 128:256], rhs=PTa[kt % 2][:, col0:TB], start=(kt == 0), stop=(kt == nkt - 1))
                                k.op("pe", lambda: T.matmul(pf[2][:, col0:TB], lhsT=ones[:, :], rhs=PTa[kt % 2][:, col0:TB], start=(kt == 0), stop=(kt == nkt - 1)),
                                     reads=[BPTa[kt % 2], Bseq], writes=[Bpf[0], Bpf[1], Bpf[2]])

                            S_(0)
                            for kt in range(nkt):
                                if kt + 1 < nkt:
                                    S_(kt + 1)
                                E_(kt)
                                PV_(kt)
                            recip_act(rd[:, :], pf[2][:, :], [Bpf[2]], [Brd])
                            k.op("dve", lambda: V.tensor_tensor(out=olat[:, 0, :], in0=pf[0][:, :], in1=rd[:, :], op=ALU.mult), reads=[Bpf[0], Brd], writes=[Bolat])
                            k.op("dve", lambda: V.tensor_tensor(out=olat[:, 1, :], in0=pf[1][:, :], in1=rd[:, :], op=ALU.mult), reads=[Bpf[1], Brd], writes=[Bolat])
                            head_out(h, TB, lambda cc: olat[:, cc, :], Bolat)
                        for h in range(4):
                            mem_q(TB, xnT, BxnT, win1, Bw1, 704, mqg1[:, 0:1], lambda hh: bb.qT[:, 0, :], bb.BqT, wk, heads=[h])
                            zgate(TB, xnT, BxnT, win1, Bw1, 2240 + h * 128, sz, Bsz)
                            mem_core(TB, bb.qT[:, 0, :], bb.BqT, KTb, Vbb, [BKVb], h, wk, sz[:, :TB], Bsz, outT[:, 8 + h, :], BoutT)
                        for ti in range(4):
                            s = ti % 2
                            k.dma("sp", xt[s][:], y1_d[b, t0 + ti * 128:t0 + (ti + 1) * 128, :], dxt[s], writes=[Bxt[s]])
                            out_proj(128, outT[:, :, ti * 128:(ti + 1) * 128], BoutT, wout1, Bw1, xt[s], Bxt[s])
                            k.dma("sp", y_p[b, t0 + ti * 128:t0 + (ti + 1) * 128, :], xt[s][:], dyo1[s], reads=[Bxt[s]])
                k.barrier()

            if "l1s" in phases:
              with ExitStack() as ss1:
                (bb, xnT, BxnT, outT, BoutT, wk, sqb, Bsqb, tt, Btt, PT, BPT, sz, Bsz, rd, Brd, cq_sb, Bcq, cqn, Bcqn, qn_sb, Bqn, qr_sb, Bqr, sqr, Bsqr,
                 qnope, Bqnope, qrb, Bqrb, ta, Bta, tb_, Btb, qabs, Bqabs, qpe, Bqpe, olat, Bolat, cosT, sinT) = alloc_l1(ss1, NS, "1s", True)
                szm = sb(ss1, "szm1", [128, 4, NS], F32); Bszm = Buf()
                ckvT_s = sb(ss1, "ckvT_s", [128, 2, NS], BF16); kpeT_s = sb(ss1, "kpeT_s", [64, NS], BF16)
                ckvtok_s = sb(ss1, "ckvtok_s", [128, KVL + 1], BF16); kss_s = sb(ss1, "kss_s", [128, NH], F32); Bseqs = Buf()
                qabs_s = sb(ss1, "qabs_s", [128, 2, BS, HT], BF16); qpe_s = sb(ss1, "qpe_s", [64, BS, HT], BF16); Bqs = Buf()
                PTn = sb(ss1, "PTn", [128, BS, HT], BF16); BPTn = Buf()
                smk = sb(ss1, "smk", [128, NS], BF16); Bsmk = Buf()
                olatT_s = sb(ss1, "olatT_s", [128, 2, NH, NS], BF16); Bols = Buf()
                olb = sb(ss1, "olb", [64, KVL], BF16); Bolb = Buf()
                rcp = sb(ss1, "rcp", [64, 1], F32); Brcp = Buf()
                ptab_sb = sb(ss1, "ptab_sb", [128, BS * NPG], I32); Bptab = Buf()
                ptf = sb(ss1, "ptf", [128, 512], F32); Bptf = Buf()
                iot = sb(ss1, "iot", [128, 1], F32)
                NSL = 6
                pgf = [sb(ss1, "pgf%d" % i, [128, 2, PW], F32) for i in range(NSL)]
                Bpgf = [Buf() for _ in range(NSL)]; dpg = [k.dsem("pg%d" % i) for i in range(NSL)]
                pgb = [sb(ss1, "pgb%d" % i, [128, 2, 321], BF16) for i in range(NSL)]; Bpgb = [Buf() for _ in range(NSL)]
                pgT = [sb(ss1, "pgT%d" % i, [128, 2, 3, 128], BF16) for i in range(3)]; BpgT = [Buf() for _ in range(3)]
                scs = [sb(ss1, "scs%d" % i, [128, 2 * HT], F32) for i in range(3)]; Bscs = [Buf() for _ in range(3)]
                PTp = [sb(ss1, "PTp%d" % i, [128, 2 * HT], BF16) for i in range(3)]; BPTp = [Buf() for _ in range(3)]
                dss = k.dsem("ss1"); dss2 = k.dsem("ss2")
                k.dma("sp", ptab_sb[:], ptab.partition_broadcast(128), dss, writes=[Bptab])
                k.op("pool", lambda: P.iota(iot[:], pattern=[[0, 1]], base=0, channel_multiplier=1, allow_small_or_imprecise_dtypes=True), writes=[Bptf])
                for c0 in range(0, BS * NPG, 512):
                    c1 = min(BS * NPG, c0 + 512)
                    k.op("pool", lambda: P.tensor_copy(out=ptf[:, :c1 - c0], in_=ptab_sb[:, c0:c1]), reads=[Bptab, Bptf], writes=[Bptf])
                    k.op("pool", lambda: P.tensor_scalar(out=ptf[:, :c1 - c0], in0=ptf[:, :c1 - c0], scalar1=128.0, scalar2=iot[:, 0:1], op0=ALU.mult, op1=ALU.add),
                         reads=[Bptf], writes=[Bptf])
                    k.op("pool", lambda: P.tensor_copy(out=ptab_sb[:, c0:c1], in_=ptf[:, :c1 - c0]), reads=[Bptf], writes=[Bptab])
                k.dma("sp", smk[:NS, :], c_smask[:, :], dss2, writes=[Bsmk])
                k.dma("sp", cosT[:, :NS], c_cosT_s[:, :], dtab, writes=[Btab])
                k.dma("sp", sinT[:, :NS], c_sinT_s[:, :], dtab, writes=[Btab])
                for i in range(NSL):
                    k.op("pool", lambda: P.memset(pgb[i][:, :, 0:1], 1.0), writes=[Bpgb[i]])
                k.op("pool", lambda: P.memset(ckvtok_s[:, 0:1], 1.0), writes=[Bseqs])
                front(None, NS, g1_b, Bw1, xnT[:, :, :NS], BxnT, keep_x=(y1s, By1s, dy1s))
                tm_tile(tws[0], NS, lambda c: xnT[:, c, :NS], c_csA_s[:, :], c_csB_s[:, :], ckv_s[:, :], kpe_s[:, :], ksc_s[:, :],
                        ckvT_s[:, :, :NS], kpeT_s[:, :NS], ckvtok_s[:NS, 1:KVL + 1], kss_s[:NS, :], Bseqs)
                cq_part(NS)
                vws = lambda a: a.rearrange("p (b t) -> p b t", b=BS)
                for h in range(NH):
                    q_head(h, NS, cosT[:, :NS], sinT[:, :NS], vws, lambda cc: qabs_s[:, cc, :, h * TS:(h + 1) * TS], qpe_s[:, :, h * TS:(h + 1) * TS], Bqs, Bqs)
                ncolq = BS * HT
                for c0 in range(0, ncolq, 512):
                    c1 = min(ncolq, c0 + 512)
                    nbc = (c1 - c0) // HT
                    b0 = c0 // HT
                    bank = 3 + (c0 // 512) % 2
                    qa = lambda cc: qabs_s[:, cc, b0:b0 + nbc, :].rearrange("p b q -> p (b q)")
                    mmg(pf[bank][:NS, :c1 - c0], Bpf[bank], [(ckvT_s[:, 0, :NS], qa(0)), (ckvT_s[:, 1, :NS], qa(1)),
                                                              (kpeT_s[:, :NS], qpe_s[:, b0:b0 + nbc, :].rearrange("p b q -> p (b q)"))], [Bseqs, Bqs])
                    pv = pf[bank][:NS, :c1 - c0].rearrange("p (b h t) -> p b h t", b=nbc, h=NH)
                    for h in range(NH):
                        k.op("act", lambda: A.activation(out=PTn[:NS, b0:b0 + nbc, h * TS:(h + 1) * TS], in_=pv[:, :, h, :], func=AF.Exp, scale=kss_s[:NS, h:h + 1]),
                             reads=[Bpf[bank], Bseqs], writes=[BPTn])
                    for h in range(NH):
                        k.op("dve", lambda: V.tensor_tensor(out=PTn[:NS, b0:b0 + nbc, h * TS:(h + 1) * TS], in0=PTn[:NS, b0:b0 + nbc, h * TS:(h + 1) * TS],
                                                             in1=smk[:NS, b0 * TS:(b0 + nbc) * TS].rearrange("p (b t) -> p b t", b=nbc), op=ALU.mult),
                             reads=[BPTn, Bsmk], writes=[BPTn])
                NPR = BS * NPG // 2
                rows = c_pool.rearrange("n p c -> (n p) c")

                def stA(i):
                    s, j, g = i % NSL, i % 2, i % 3
                    if not k.alive():
                        return
                    k.n += 1
                    k.deps("pool", [Bptab], [Bpgf[s]])
                    for a in range(2):
                        off = bass.IndirectOffsetOnAxis(ap=ptab_sb[:, 2 * i + a:2 * i + a + 1], axis=0)
                        P.indirect_dma_start(out=pgf[s][:, a, :], out_offset=None, in_=rows, in_offset=off).then_inc(dpg[s][0], 16)
                    dpg[s][1][0] += 32
                    k.done((dpg[s][0], dpg[s][1][0], dpg[s][2]), [Bptab], [Bpgf[s]])
                    k.op("dve", lambda: V.tensor_copy(out=pgb[s][:, :, 1:321], in_=pgf[s][:, :, 0:320]), reads=[Bpgf[s]], writes=[Bpgb[s]])
                    k.deps("pe", [Bpgb[s]], [Bpt[j]])
                    if k.alive():
                        for a in range(2):
                            T.transpose(out=pt[j][:, a * 384:a * 384 + 128], in_=pgb[s][:, a, 1:129], identity=ident[:])
                            T.transpose(out=pt[j][:, a * 384 + 128:a * 384 + 256], in_=pgb[s][:, a, 129:257], identity=ident[:])
                            if a == 0:
                                T.transpose(out=pt[j][0:64, a * 384 + 256:a * 384 + 384], in_=pgb[s][:, a, 257:321], identity=ident[:])
                    k.op("pe", lambda: T.transpose(out=pt[j][0:64, 384 + 256:384 + 384], in_=pgb[s][:, 1, 257:321], identity=ident[:]), reads=[Bpgb[s]], writes=[Bpt[j]])
                    k.op("act", lambda: A.copy(out=pgT[g][:].rearrange("p a c t -> p (a c t)"), in_=pt[j][:, 0:768]), reads=[Bpt[j]], writes=[BpgT[g]])

                def stB(i):
                    b = (2 * i) // NPG
                    s, g, j = i % NSL, i % 3, i % 3
                    bank = 3 + i % 2
                    for a in range(2):
                        mmg(pf[bank][:, a * HT:(a + 1) * HT], Bpf[bank], [(pgT[g][:, a, 0, :], qabs_s[:, 0, b, :]), (pgT[g][:, a, 1, :], qabs_s[:, 1, b, :]),
                                                                         (pgT[g][0:64, a, 2, :], qpe_s[:, b, :])], [BpgT[g], Bqs])
                    k.op("dve", lambda: V.tensor_tensor(out=scs[j][:, :].rearrange("p (a h t) -> p a h t", a=2, h=NH),
                                                        in0=pf[bank][:, 0:2 * HT].rearrange("p (a h t) -> p a h t", a=2, h=NH),
                                                        in1=pgf[s][:, :, KVL + ROPE:PW].unsqueeze(3).to_broadcast([128, 2, NH, TS]), op=ALU.mult),
                         reads=[Bpf[bank], Bpgf[s]], writes=[Bscs[j]])
                    k.op("act", lambda: A.activation(out=PTp[j][:, :], in_=scs[j][:, :], func=AF.Exp, scale=SM_SCALE), reads=[Bscs[j]], writes=[BPTp[j]])

                def stC(i):
                    b, pj = (2 * i) // NPG, (2 * i) % NPG
                    s, j = i % NSL, i % 3
                    acc = b % 2
                    if pj == 0:
                        mmg(pf[acc][:HT, 0:KVL + 1], Bpf[acc], [(PTn[:NS, b, :], ckvtok_s[:NS, :])], [BPTn, Bseqs], start=True, stop=False)
                    last = (pj == NPG - 2)
                    mmg(pf[acc][:HT, 0:KVL + 1], Bpf[acc], [(PTp[j][:, 0:HT], pgb[s][:, 0, 0:KVL + 1]), (PTp[j][:, HT:2 * HT], pgb[s][:, 1, 0:KVL + 1])],
                        [BPTp[j], Bpgb[s]], start=False, stop=last)
                    if last:
                        k.op("dve", lambda: V.reciprocal(out=rcp[:, :], in_=pf[acc][:HT, 0:1]), reads=[Bpf[acc]], writes=[Brcp])
                        k.op("dve", lambda: V.tensor_scalar(out=olb[:, :], in0=pf[acc][:HT, 1:KVL + 1], scalar1=rcp[:, 0:1], scalar2=None, op0=ALU.mult),
                             reads=[Bpf[acc], Brcp], writes=[Bolb])
                        jj = b % 2
                        k.deps("pe", [Bolb], [Bpt[jj]])
                        if k.alive():
                            T.transpose(out=pt[jj][:, 768:768 + HT], in_=olb[:, 0:128], identity=ident[:HT, :HT])
                        k.op("pe", lambda: T.transpose(out=pt[jj][:, 768 + HT:768 + 2 * HT], in_=olb[:, 128:256], identity=ident[:HT, :HT]), reads=[Bolb], writes=[Bpt[jj]])
                        k.op("act", lambda: A.copy(out=olatT_s[:, :, :, b * TS:(b + 1) * TS], in_=pt[jj][:, 768:768 + 2 * HT].rearrange("p (c h t) -> p c h t", c=2, h=NH)),
                             reads=[Bpt[jj]], writes=[Bols])

                for step in range(NPR + 4):
                    if step < NPR:
                        stA(step)
                    if 2 <= step <= NPR + 1:
                        stB(step - 2)
                    if 4 <= step <= NPR + 3:
                        stC(step - 4)
                for h in range(NH):
                    head_out(h, NS, lambda cc: olatT_s[:, cc, h, :NS], Bols)
                mem_q(NS, xnT, BxnT, win1, Bw1, 704, mqg1[:, 0:1], lambda hh: bb.qT[:, hh, :NS], bb.BqT, wk)
                for h in range(4):
                    zgate(NS, xnT, BxnT, win1, Bw1, 2240 + h * 128, szm[:, h, :], Bszm)
                sample_mem_attn(1, NS, bb.qT, bb.BqT, szm, Bszm, outT, BoutT, wk, ss1)
                out_proj(NS, outT, BoutT, wout1, Bw1, y1s, By1s)
                k.dma("sp", y_s[:, :], y1s[:NS, :], dy1s, reads=[By1s])
                k.barrier()
            k.barrier()
        k.barrier()
        print("K ops:", k.n, "cnt", k.cnt)
    return nc


def _consts(cfg):
    S, NS, BS = cfg.seq, cfg.ns, cfg.bs_s
    bf = ml_dtypes.bfloat16
    inv = (1.0 / (ROPE_THETA ** (np.arange(0, ROPE, 2, dtype=np.float32) / ROPE))).astype(np.float32)

    def tabs(pos):
        ang = pos.astype(np.float32)[:, None] * inv[None, :]
        ang = ang.astype(np.float32)
        cos, sin = np.cos(ang).astype(np.float32), np.sin(ang).astype(np.float32)
        return cos, sin
    cp, sp = tabs(np.arange(S))
    past = cfg.npages * PAGE
    cs_, ss_ = tabs(past + np.arange(TS))
    cs_, ss_ = np.tile(cs_, (BS, 1)), np.tile(ss_, (BS, 1))
    rot = np.zeros((64, 64), np.float32)
    for i in range(32):
        rot[i, i + 32] = -1.0
        rot[i + 32, i] = 1.0
    kk = np.arange(128)
    mask = (kk[:, None] <= kk[None, :]).astype(np.float32)
    kb, kt = np.arange(NS) // TS, np.arange(NS) % TS
    smask = ((kb[:, None] == kb[None, :]) & (kt[:, None] <= kt[None, :])).astype(np.float32)
    return {
        "c_ident": np.eye(128).astype(bf), "c_identf": np.eye(128, dtype=np.float32), "c_mask": mask.astype(bf),
        "c_rot": rot.T.copy().astype(bf),
        "c_csA_p": np.concatenate([cp, cp], 1), "c_csB_p": np.concatenate([-sp, sp], 1),
        "c_csA_s": np.concatenate([cs_, cs_], 1), "c_csB_s": np.concatenate([-ss_, ss_], 1),
        "c_cosT_p": np.ascontiguousarray(np.concatenate([cp, cp], 1).T), "c_sinT_p": np.ascontiguousarray(np.concatenate([sp, sp], 1).T),
        "c_cosT_s": np.ascontiguousarray(np.concatenate([cs_, cs_], 1).T), "c_sinT_s": np.ascontiguousarray(np.concatenate([ss_, ss_], 1).T),
        "c_smask": smask.astype(bf),
    }


def run(cfg, inputs, phases=("m", "l0p", "l0s", "l1p", "l1s"), trace=False):
    NB, BS, NC = cfg.nb_p, cfg.bs_s, cfg.ncores
    f = lambda a: np.ascontiguousarray(np.asarray(a))
    consts = _consts(cfg)
    nc = build(cfg, phases)
    row = lambda a: f(a).reshape(1, -1)
    shared = {
        "c_pool": np.concatenate([np.asarray(inputs["cache_ckv"]), np.asarray(inputs["cache_kpe"]), np.asarray(inputs["cache_kscale"])], axis=2),
        "conv_norm_g": row(inputs["conv_norm_g"]), "conv_w_in": f(inputs["conv_w_in"]), "conv_w": f(inputs["conv_w"]),
        "conv_w_out": f(inputs["conv_w_out"]), "mla_norm_g": row(inputs["mla_norm_g"]), "mla_w_in": f(inputs["mla_w_in"]),
        "mla_q_lora_g": row(inputs["mla_q_lora_g"]), "mla_w_uq": f(inputs["mla_w_uq"]), "mla_ckv_g": row(inputs["mla_ckv_g"]),
        "mla_w_uk": f(inputs["mla_w_uk"]).reshape(KVL, 1024), "mla_w_uv": f(inputs["mla_w_uv"]).reshape(KVL, 1024),
        "mla_q_g": row(inputs["mla_q_g"]), "mla_k_g": row(inputs["mla_k_g"]), "mla_w_out": f(inputs["mla_w_out"]),
        "mem_norm_g": f(inputs["mem_norm_g"]), "mem_w_kv": f(inputs["mem_w_kv"]), "mem_q_g": f(inputs["mem_q_g"]), "mem_k_g": f(inputs["mem_k_g"]),
    }
    shared.update(consts)
    in_maps = []
    for c in range(NC):
        m = dict(shared)
        m["x_p"] = f(inputs["x_prompt"][c * NB:(c + 1) * NB])
        m["x_s"] = f(inputs["x_sample"][c * BS:(c + 1) * BS]).reshape(BS * TS, D)
        m["mem_p"] = f(inputs["mem_prompt"][c * NB:(c + 1) * NB])
        m["st_conv"] = f(inputs["state_conv"][c * BS:(c + 1) * BS]).reshape(BS * 2, D)
        m["c_mk"] = f(inputs["cache_mem_k"][:, c * BS:(c + 1) * BS]).reshape(2, BS, NMEM, 512)
        m["c_mv"] = f(inputs["cache_mem_v"][:, c * BS:(c + 1) * BS]).reshape(2, BS, NMEM, 512)
        m["ptab"] = f(inputs["page_table"][c * BS:(c + 1) * BS]).reshape(1, -1).astype(np.int32)
        in_maps.append(m)
    res = run_bass_kernel_spmd(nc, in_maps, core_ids=list(range(NC)), trace=trace)
    R = res.results
    S = cfg.seq
    cat = lambda name, shp: np.concatenate([R[c][name].reshape(shp) for c in range(NC)], axis=0)
    outs = (
        cat("y_p", (NB, S, D)), cat("y_s", (BS, TS, D)), cat("nconv_p", (NB, 2, D)), cat("nconv_s", (BS, 2, D)),
        cat("ckv_p", (NB, S, KVL)), cat("kpe_p", (NB, S, ROPE)), cat("ksc_p", (NB, S, NH)),
        cat("ckv_s", (BS, TS, KVL)), cat("kpe_s", (BS, TS, ROPE)), cat("ksc_s", (BS, TS, NH)),
        np.concatenate([R[c]["nmk_p"].reshape(2, NB, NMEM, MH, 128) for c in range(NC)], axis=1),
        np.concatenate([R[c]["nmv_p"].reshape(2, NB, NMEM, MH, 128) for c in range(NC)], axis=1),
    )
    return outs, res


def kernel(**inputs):
    cfg = Cfg()
    outs, _ = run(cfg, inputs)
    return tuple(np.asarray(o, dtype=np.float32) for o in outs)
```
